# Optimizing a Trainium2 kernel written in Bass

```python
import math
import jax, jax.numpy as jnp
from jax import lax
import numpy as np

D_MODEL = 1024
BATCH = 8
SEQ = 2048
DEPTH = 4

N_MIXERS = 2
EXPAND = 2
D_INNER = EXPAND * D_MODEL
EPS = 1e-6
GLA_HEADS = 4
GLA_KEY_DIM = D_MODEL // 2
GLA_DK = GLA_KEY_DIM // GLA_HEADS
GLA_DV = D_INNER // GLA_HEADS
GLA_GATE_RANK = 16
GLA_GATE_TAU = 16.0
GLA_CHUNK = 64
GLA_IN = 2 * GLA_KEY_DIM + 2 * D_INNER + GLA_GATE_RANK
S5_GROUP = 16
S5_GROUPS = D_INNER // S5_GROUP
S5_STATE = 64
S5_DT_MIN = 1e-3
S5_DT_MAX = 1e-1
S5_IN = 2 * D_INNER
N_GLA = (DEPTH + 1) // 2
N_S5 = DEPTH // 2

kernel_name = "hybrid_gla_s5_interleaved"


def _rmsnorm(x, g):
    x32 = x.astype(jnp.float32)
    y = x32 * lax.rsqrt(jnp.mean(x32 * x32, axis=-1, keepdims=True) + EPS) * g.astype(jnp.float32)
    return y.astype(x.dtype)


def _gla_mixer(h, w_in, w_gate_up, b_gate, head_gain, w_out):
    bsz, seqlen, _ = h.shape
    n_chunks = seqlen // GLA_CHUNK
    f32 = jnp.float32
    proj = h @ w_in
    q, k, v, z, r = jnp.split(
        proj,
        [GLA_KEY_DIM, 2 * GLA_KEY_DIM, 2 * GLA_KEY_DIM + D_INNER, 2 * GLA_KEY_DIM + 2 * D_INNER],
        axis=-1)
    log_a = jax.nn.log_sigmoid((r @ w_gate_up + b_gate).astype(f32)) / GLA_GATE_TAU

    def to_chunks(t, d):
        return t.astype(f32).reshape(bsz, n_chunks, GLA_CHUNK, GLA_HEADS, d).transpose(0, 3, 1, 2, 4)

    qc = to_chunks(q, GLA_DK) * (GLA_DK ** -0.5)
    kc = to_chunks(k, GLA_DK)
    vc = to_chunks(v, GLA_DV)
    bcum = jnp.cumsum(to_chunks(log_a, GLA_DK), axis=3)
    b_last = bcum[:, :, :, -1:, :]

    q_g = qc * jnp.exp(bcum)
    k_g = kc * jnp.exp(-bcum)
    k_end = kc * jnp.exp(b_last - bcum)

    causal = jnp.tril(jnp.ones((GLA_CHUNK, GLA_CHUNK), dtype=bool))
    att = jnp.einsum('bhncd,bhnsd->bhncs', q_g, k_g)
    att = jnp.where(causal, att, 0.0)
    o_intra = jnp.einsum('bhncs,bhnse->bhnce', att, vc)

    kv_chunk = jnp.einsum('bhnsd,bhnse->bhnde', k_end, vc)
    decay_chunk = jnp.exp(b_last[:, :, :, 0, :])

    def step(state, inp):
        dec, kv = inp
        return dec[..., None] * state + kv, state

    s0 = jnp.zeros((bsz, GLA_HEADS, GLA_DK, GLA_DV), f32)
    _, s_prev = lax.scan(step, s0, (jnp.moveaxis(decay_chunk, 2, 0), jnp.moveaxis(kv_chunk, 2, 0)))
    s_prev = jnp.moveaxis(s_prev, 0, 2)
    o = o_intra + jnp.einsum('bhncd,bhnde->bhnce', q_g, s_prev)

    o = o.transpose(0, 2, 3, 1, 4).reshape(bsz, seqlen, GLA_HEADS, GLA_DV)
    o = o * lax.rsqrt(jnp.mean(o * o, axis=-1, keepdims=True) + EPS) * head_gain.astype(f32)
    o = o.reshape(bsz, seqlen, D_INNER) * jax.nn.silu(z.astype(f32))
    return o.astype(h.dtype) @ w_out


def _s5_mixer(h, w_in, lam_re, lam_im, log_dt, b_re, b_im, c_re, c_im, d_skip, w_glu, b_glu, w_out):
    bsz, seqlen, _ = h.shape
    f32 = jnp.float32
    proj = h @ w_in
    u, z = jnp.split(proj, [D_INNER], axis=-1)
    u32 = u.astype(f32).reshape(bsz, seqlen, S5_GROUPS, S5_GROUP)

    lam = lax.complex(lam_re.astype(f32), lam_im.astype(f32))
    dt = jnp.exp(log_dt.astype(f32))[:, None]
    lam_bar = jnp.exp(lam * dt)
    b_coef = (lam_bar - 1.0) / lam
    b_cplx = lax.complex(b_re.astype(f32), b_im.astype(f32))
    b_bar = b_coef[..., None] * b_cplx
    bu = lax.complex(jnp.einsum('blgi,gpi->blgp', u32, jnp.real(b_bar)),
                     jnp.einsum('blgi,gpi->blgp', u32, jnp.imag(b_bar)))
    a = jnp.broadcast_to(lam_bar[None, None], (1, seqlen, S5_GROUPS, S5_STATE))

    def combine(left, right):
        a_l, b_l = left
        a_r, b_r = right
        return a_l * a_r, a_r * b_l + b_r

    _, states = lax.associative_scan(combine, (a, bu), axis=1)
    y = (jnp.einsum('blgp,gip->blgi', jnp.real(states), c_re.astype(f32))
         - jnp.einsum('blgp,gip->blgi', jnp.imag(states), c_im.astype(f32)))
    y = y + d_skip.astype(f32) * u32
    y = jax.nn.gelu(y.reshape(bsz, seqlen, D_INNER))
    y = y * jax.nn.sigmoid(y @ w_glu.astype(f32) + b_glu.astype(f32))
    y = y * jax.nn.silu(z.astype(f32))
    return y.astype(h.dtype) @ w_out


def setup_inputs(seed: int = 0) -> dict:
    key = jax.random.key(seed)
    ks = jax.random.split(key, 24)
    nrm = jax.random.normal
    f32 = jnp.float32
    x = nrm(ks[0], (BATCH, SEQ, D_MODEL), f32)
    gla_norm = 1.0 + 0.01 * nrm(ks[1], (N_GLA, D_MODEL), f32)
    gla_w_in = nrm(ks[2], (N_GLA, D_MODEL, GLA_IN), f32) * D_MODEL ** -0.5
    gla_w_gate_up = nrm(ks[3], (N_GLA, GLA_GATE_RANK, GLA_KEY_DIM), f32) * GLA_GATE_RANK ** -0.5
    gla_b_gate = 0.1 * nrm(ks[4], (N_GLA, GLA_KEY_DIM), f32)
    gla_head_gain = 1.0 + 0.01 * nrm(ks[5], (N_GLA, GLA_DV), f32)
    gla_w_out = nrm(ks[6], (N_GLA, D_INNER, D_MODEL), f32) * D_INNER ** -0.5
    s5_norm = 1.0 + 0.01 * nrm(ks[7], (N_S5, D_MODEL), f32)
    s5_w_in = nrm(ks[8], (N_S5, D_MODEL, S5_IN), f32) * D_MODEL ** -0.5
    s5_lam_re = -0.5 + 1e-3 * nrm(ks[9], (N_S5, S5_GROUPS, S5_STATE), f32)
    s5_lam_im = (math.pi * jnp.arange(S5_STATE, dtype=f32))[None, None, :] \
        + 1e-3 * nrm(ks[10], (N_S5, S5_GROUPS, S5_STATE), f32)
    s5_log_dt = jax.random.uniform(ks[11], (N_S5, S5_GROUPS), f32,
                                   minval=math.log(S5_DT_MIN), maxval=math.log(S5_DT_MAX))
    s5_b_re = nrm(ks[12], (N_S5, S5_GROUPS, S5_STATE, S5_GROUP), f32) * (2 * S5_GROUP) ** -0.5
    s5_b_im = nrm(ks[13], (N_S5, S5_GROUPS, S5_STATE, S5_GROUP), f32) * (2 * S5_GROUP) ** -0.5
    s5_c_re = nrm(ks[14], (N_S5, S5_GROUPS, S5_GROUP, S5_STATE), f32) * S5_STATE ** -0.5
    s5_c_im = nrm(ks[15], (N_S5, S5_GROUPS, S5_GROUP, S5_STATE), f32) * S5_STATE ** -0.5
    s5_d = nrm(ks[16], (N_S5, S5_GROUPS, S5_GROUP), f32)
    s5_w_glu = nrm(ks[17], (N_S5, D_INNER, D_INNER), f32) * D_INNER ** -0.5
    s5_b_glu = 0.01 * nrm(ks[18], (N_S5, D_INNER), f32)
    s5_w_out = nrm(ks[19], (N_S5, D_INNER, D_MODEL), f32) * D_INNER ** -0.5
    final_norm = 1.0 + 0.01 * nrm(ks[20], (D_MODEL,), f32)
    return {
        "x": x,
        "gla_norm": gla_norm, "gla_w_in": gla_w_in, "gla_w_gate_up": gla_w_gate_up,
        "gla_b_gate": gla_b_gate, "gla_head_gain": gla_head_gain, "gla_w_out": gla_w_out,
        "s5_norm": s5_norm, "s5_w_in": s5_w_in, "s5_lam_re": s5_lam_re, "s5_lam_im": s5_lam_im,
        "s5_log_dt": s5_log_dt, "s5_b_re": s5_b_re, "s5_b_im": s5_b_im,
        "s5_c_re": s5_c_re, "s5_c_im": s5_c_im, "s5_d": s5_d,
        "s5_w_glu": s5_w_glu, "s5_b_glu": s5_b_glu, "s5_w_out": s5_w_out,
        "final_norm": final_norm,
    }


def reference(x, gla_norm, gla_w_in, gla_w_gate_up, gla_b_gate, gla_head_gain, gla_w_out,
              s5_norm, s5_w_in, s5_lam_re, s5_lam_im, s5_log_dt, s5_b_re, s5_b_im,
              s5_c_re, s5_c_im, s5_d, s5_w_glu, s5_b_glu, s5_w_out, final_norm):
    for i in range(DEPTH):
        j = i // N_MIXERS
        if i % N_MIXERS == 0:
            h = _rmsnorm(x, gla_norm[j])
            out = _gla_mixer(h, gla_w_in[j], gla_w_gate_up[j], gla_b_gate[j],
                             gla_head_gain[j], gla_w_out[j])
        else:
            h = _rmsnorm(x, s5_norm[j])
            out = _s5_mixer(h, s5_w_in[j], s5_lam_re[j], s5_lam_im[j], s5_log_dt[j],
                            s5_b_re[j], s5_b_im[j], s5_c_re[j], s5_c_im[j], s5_d[j],
                            s5_w_glu[j], s5_b_glu[j], s5_w_out[j])
        x = x + out.astype(x.dtype)
    return _rmsnorm(x, final_norm)
```

```python
import contextlib
import math
import numpy as np
import concourse.bass as bass
import concourse.mybir as mybir
from concourse.bass_utils import run_bass_kernel_spmd

F32 = mybir.dt.float32
BF16 = mybir.dt.bfloat16
ALU = mybir.AluOpType
AF = mybir.ActivationFunctionType

D = 1024
L = 2048
DI = 2048
EPS = 1e-6
NT = L // 128
GLA_IN = 5136
N_CORES = 8


class Buf:
    __slots__ = ("name", "w", "r")

    def __init__(self, name):
        self.name = name
        self.w = None
        self.r = []


class Op:
    __slots__ = ("eng", "fn", "deps", "sem", "cnt", "dma", "pre", "small")


class Prog:
    ENGS = ("pe", "act", "dve", "pool", "sp")

    def __init__(self, nc, es, n_dma_sems=32, same_engine=True):
        self.nc = nc
        self.same_engine = same_engine
        self.ops = {e: [] for e in self.ENGS}
        self.esem = {e: es.enter_context(nc.semaphore("sem_" + e)) for e in self.ENGS}
        self.ecnt = {e: 0 for e in self.ENGS}
        self.dsem = [es.enter_context(nc.semaphore("dsem%d" % i)) for i in range(n_dma_sems)]
        self.dtot = [0] * n_dma_sems
        self.dlast = [None] * n_dma_sems
        self.dnext = 0
        self.bar = []

    def barrier(self):
        deps = []
        for e in self.ENGS:
            for o in reversed(self.ops[e]):
                if not o.dma:
                    deps.append(o)
                    break
        for o in self.dlast:
            if o is not None:
                deps.append(o)
        self.bar = deps

    def op(self, eng, fn, reads=(), writes=(), dma=False, small=False):
        o = Op()
        o.small = small
        o.eng = eng
        o.fn = fn
        o.dma = dma
        o.pre = None
        deps = list(self.bar)
        for b in reads:
            if b.w is not None:
                deps.append(b.w)
        for b in writes:
            if b.w is not None:
                deps.append(b.w)
            deps.extend(b.r)
        for b in reads:
            b.r.append(o)
        for b in writes:
            b.w = o
            b.r = []
        if dma:
            j = self.dnext
            self.dnext = (j + 1) % len(self.dsem)
            o.pre = self.dlast[j]
            self.dtot[j] += 16
            o.sem = self.dsem[j]
            o.cnt = self.dtot[j]
            self.dlast[j] = o
        else:
            self.ecnt[eng] += 1
            o.sem = self.esem[eng]
            o.cnt = self.ecnt[eng]
        if eng == "pe" and self.ops["pe"]:
            prev = self.ops["pe"][-1]
            if small or prev.small:
                deps.append(prev)
        o.deps = deps
        self.ops[eng].append(o)
        return o

    def flush(self, block, finish=False):
        prog = self
        if not hasattr(self, "waited"):
            self.waited = {e: {} for e in self.ENGS}
            self.done = {e: 0 for e in self.ENGS}

        def run(eng_name, e):
            waited = prog.waited[eng_name]
            ops = prog.ops[eng_name]
            for o in ops[prog.done[eng_name]:]:
                need = {}
                dl = o.deps
                if o.pre is not None:
                    dl = dl + [o.pre]
                for d in dl:
                    if d is o:
                        continue
                    if d.eng == eng_name and not d.dma and (not prog.same_engine):
                        continue
                    if eng_name == "pe" and d.eng == "pe" and not (o.small or d.small):
                        continue
                    k = id(d.sem)
                    if k not in need or need[k][1] < d.cnt:
                        need[k] = (d.sem, d.cnt)
                for k, (sem, cnt) in need.items():
                    if waited.get(k, 0) >= cnt:
                        continue
                    e.wait_ge(sem, cnt)
                    waited[k] = cnt
                ins = o.fn(e)
                ins.then_inc(o.sem, 16 if o.dma else 1)
                o.fn = None
                o.deps = None
            prog.done[eng_name] = len(ops)
            if finish:
                last = {}
                for o in ops:
                    if o.dma:
                        last[id(o.sem)] = o
                for k, o in last.items():
                    if waited.get(k, 0) < o.cnt:
                        e.wait_ge(o.sem, o.cnt)
                        waited[k] = o.cnt

        block.tensor(lambda e: run("pe", e))
        block.scalar(lambda e: run("act", e))
        block.vector(lambda e: run("dve", e))
        block.gpsimd(lambda e: run("pool", e))
        block.sync(lambda e: run("sp", e))


class TT:
    def __init__(self, h, n, name, nparts=1):
        self.h = h
        self.n = n
        self.b = Buf(name)
        self.parts = [Buf("%s.%d" % (name, i)) for i in range(nparts)]

    def v(self, off=0, dims=None, p0=0, pn=128):
        if dims is None:
            dims = [[1, self.n - off]]
        return bass.AP(self.h, p0 * self.n + off, [[self.n, pn]] + [list(d) for d in dims])


class Ctx:
    def __init__(self, nc, es, P):
        self.nc = nc
        self.es = es
        self.P = P
        self.banks = []
        self.bank_i = 0
        self.bank_cls = {"A": [0, 1], "B": [2, 3], "O": [4, 5], "C": [6, 7]}
        self.bank_ci = {}

    def sb(self, es, name, n, dt, nparts=1):
        self.uid = getattr(self, "uid", 0) + 1
        name = "%s_u%d" % (name, self.uid)
        h = es.enter_context(self.nc.sbuf_tensor(name, [128, n], dt))
        return TT(h, n, name, nparts)

    def dump(self, tt, name, dt=None):
        if not getattr(self, "dbg", None):
            return
        dt = dt or F32
        if ("dbg_" + name) in self.dbg_names:
            return
        d = self.nc.dram_tensor("dbg_" + name, [128, tt.n], dt, kind="ExternalOutput")
        self.dbg_names.append("dbg_" + name)
        self.P.op("sp", lambda e: e.dma_start(out=bass.AP(d, 0, [[tt.n, 128], [1, tt.n]]), in_=tt.v()),
                  reads=[tt.b] + tt.parts, dma=True)

    def bank(self, cls=None):
        if cls is None:
            b = self.banks[self.bank_i]
            self.bank_i = (self.bank_i + 1) % len(self.banks)
            return b
        ids = self.bank_cls[cls]
        k = self.bank_ci.get(cls, 0)
        self.bank_ci[cls] = (k + 1) % len(ids)
        return self.banks[ids[k]]


def _bf_ap(bank, off, dims, p0, pn):
    base = bass.AP(bank.h, 0, [[512, 128], [1, 512]]).bitcast(BF16)
    return bass.AP(base.tensor, p0 * 1024 + off, [[1024, pn]] + [list(d) for d in dims])


def rms_prep(C, xt, xn, ssq, var, sd, rstd, n_feat):
    P = C.P
    nh = C.consts["neghalf"]
    P.op("act", lambda e: e.activation(out=xn.v(), in_=xt.v(), func=AF.Square, accum_out=ssq.v()),
         reads=[xt.b], writes=[xn.b, ssq.b])
    P.op("dve", lambda e: e.tensor_scalar(out=var.v(), in0=ssq.v(), scalar1=1.0 / n_feat, scalar2=EPS,
                                          op0=ALU.mult, op1=ALU.add), reads=[ssq.b], writes=[var.b])
    P.op("pool", lambda e: e.tensor_tensor(out=rstd.v(), in0=var.v(), in1=nh.v(0, [[1, 1]]), op=ALU.pow),
         reads=[var.b, nh.b], writes=[rstd.b])
    P.op("act", lambda e: e.activation(out=xn.v(), in_=xt.v(), func=AF.Copy, scale=rstd.v()),
         reads=[xt.b, rstd.b], writes=[xn.b])


def make_hT(C, xn, hT, gcol, ident, cls=None):
    P = C.P
    for half in range(2):
        bk = C.bank(cls)
        for q in range(4):
            kc = half * 4 + q
            P.op("pe", lambda e, kc=kc, q=q, bk=bk: e.transpose(out=bk.v(q * 128, [[1, 128]]),
                                                                 in_=xn.v(kc * 128, [[1, 128]]),
                                                                 identity=ident.v()),
                 reads=[xn.b, ident.b], writes=[bk.b])
        P.op("dve", lambda e, half=half, bk=bk: e.tensor_tensor(
            out=hT.v(half * 512, [[128, 4], [1, 128]]), in0=bk.v(0, [[128, 4], [1, 128]]),
            in1=gcol.v(half * 4, [[1, 4], [0, 128]]), op=ALU.mult),
            reads=[bk.b, gcol.b], writes=[hT.parts[half]])


def load_w_cast(C, dst, dram_ap_fn, rows_chunks, ncols, col0, total_cols, src_cols):
    P = C.P
    step = 1024
    for kc in range(rows_chunks):
        for c0 in range(0, ncols, step):
            cn = min(step, ncols - c0)
            P.op("pool", lambda e, kc=kc, c0=c0, cn=cn: e.dma_start(
                out=dst.v(kc * ncols + c0, [[1, cn]]),
                in_=dram_ap_fn(kc * 128, col0 + c0, cn)),
                writes=[dst.b], dma=True)


def interleave(streams):
    live = list(streams)
    while live:
        nxt = []
        for g in live:
            try:
                next(g)
                nxt.append(g)
            except StopIteration:
                pass
        live = nxt


def gla_layer(C, es0, x_src, x_dst, W, consts, final=None):
    nc, P = C.nc, C.P
    es = es0.enter_context(contextlib.ExitStack())
    sb = lambda name, n, dt, nparts=1: C.sb(es, name, n, dt, nparts)
    ident, tri, mask01, ones = consts["ident"], consts["tri"], consts["mask"], consts["ones"]
    identb = consts["identb"]
    nh = consts["neghalf"]

    Wq = sb("Wq", 8 * 512, BF16)
    Wk = sb("Wk", 8 * 512, BF16)
    Wv = sb("Wv", 8 * 2048, BF16)
    Wz = sb("Wz", 8 * 2048, BF16)
    Wr = sb("Wr", 8 * 16, BF16)
    Wo = sb("Wo", 16 * 1024, BF16)
    Wg = sb("Wg", 512, F32)
    bg = sb("bg", 512, F32)
    gcol = sb("gcol", 8, F32)
    gain = sb("gain", 512, F32)
    S = sb("S", 4 * 512, F32, nparts=4)
    Sbf = sb("Sbf", 4 * 512, BF16, nparts=4)
    xts = [sb("xt%d" % i, 1024, F32) for i in range(2)]
    xn = sb("xn", 1024, F32)
    hT = sb("hT", 1024, BF16, nparts=2)
    vts = [sb("vt%d" % i, 2048, BF16, nparts=4) for i in range(2)]
    szs = [sb("sz%d" % i, 2048, F32, nparts=4) for i in range(2)]
    qks = [sb("qk%d" % i, 1024, F32, nparts=2) for i in range(2)]
    rTss = [sb("rTs%d" % i, 128, F32) for i in range(2)]
    lsp = sb("lsp", 512, F32)
    E1 = sb("E1", 512, F32)
    E2 = sb("E2", 512, F32)
    qg = sb("qg", 512, BF16)
    kg = sb("kg", 512, BF16)
    kendT = sb("kendT", 512, BF16)
    att = sb("att", 512, BF16)
    kend = sb("kend", 512, BF16)
    og = sb("og", 2048, BF16, nparts=4)
    ogT = sb("ogT", 2048, BF16, nparts=4)
    xo = sb("xo", 1024, F32)
    junk = sb("junk", 512, BF16)
    ssq = sb("ssq", 1, F32)
    var = sb("var", 1, F32)
    sd = None
    rstd = sb("rstd", 1, F32)
    ssqo = sb("ssqo", 4, F32, nparts=4)
    varo = sb("varo", 4, F32, nparts=4)
    rstdo = sb("rstdo", 4, F32, nparts=4)
    if final is not None:
        fng = sb("fng", 1024, F32)
        yo = sb("yo", 1024, F32)
        ssq2 = sb("ssq2", 1, F32)
        var2 = sb("var2", 1, F32)
        rstd2 = sb("rstd2", 1, F32)

    win = W["w_in"]
    wap = lambda r0, c0, cn: bass.AP(win, r0 * GLA_IN + c0, [[GLA_IN, 128], [1, cn]])
    load_w_cast(C, Wv, wap, 8, 2048, 1024, None, None)
    load_w_cast(C, Wz, wap, 8, 2048, 3072, None, None)
    load_w_cast(C, Wq, wap, 8, 512, 0, None, None)
    load_w_cast(C, Wk, wap, 8, 512, 512, None, None)
    load_w_cast(C, Wr, wap, 8, 16, 5120, None, None)
    wout = W["w_out"]
    woap = lambda r0, c0, cn: bass.AP(wout, r0 * 1024 + c0, [[1024, 128], [1, cn]])
    load_w_cast(C, Wo, woap, 16, 1024, 0, None, None)
    P.op("sp", lambda e: e.dma_start(out=Wg.v(0, [[1, 512]], 0, 16), in_=bass.AP(W["w_gate_up"], 0, [[512, 16], [1, 512]])),
         writes=[Wg.b], dma=True)
    P.op("sp", lambda e: e.dma_start(out=bg.v(0, [[1, 512]], 0, 1), in_=bass.AP(W["b_gate"], 0, [[512, 1], [1, 512]])),
         writes=[bg.b], dma=True)
    P.op("sp", lambda e: e.dma_start(out=gcol.v(), in_=bass.AP(W["norm_col"], 0, [[8, 128], [1, 8]])),
         writes=[gcol.b], dma=True)
    P.op("sp", lambda e: e.dma_start(out=gain.v(), in_=bass.AP(W["gain_rep"], 0, [[512, 128], [1, 512]])),
         writes=[gain.b], dma=True)
    if final is not None:
        P.op("sp", lambda e: e.dma_start(out=fng.v(), in_=bass.AP(final, 0, [[1024, 128], [1, 1024]])),
             writes=[fng.b], dma=True)
    P.op("dve", lambda e: e.memset(S.v(), 0.0), writes=[S.b] + S.parts)
    P.op("dve", lambda e: e.memset(Sbf.v(), 0.0), writes=[Sbf.b] + Sbf.parts)
    hTr = [hT.parts[0], hT.parts[1]]

    def stageA(t):
        xt, vt, sz, qk, rTs = xts[t % 2], vts[t % 2], szs[t % 2], qks[t % 2], rTss[t % 2]
        P.op("sp", lambda e: e.dma_start(out=xt.v(), in_=bass.AP(x_src, t * 128 * 1024, [[1024, 128], [1, 1024]])),
             reads=[], writes=[xt.b], dma=True)
        rms_prep(C, xt, xn, ssq, var, sd, rstd, D)
        yield
        make_hT(C, xn, hT, gcol, ident, cls="A")
        yield
        for cb in range(4):
            bk = C.bank("A")
            for kc in range(8):
                P.op("pe", lambda e, kc=kc, cb=cb, bk=bk: e.matmul(
                    bk.v(), lhsT=hT.v(kc * 128, [[1, 128]]), rhs=Wv.v(kc * 2048 + cb * 512, [[1, 512]]),
                    start=(kc == 0), stop=(kc == 7)), reads=hTr + [Wv.b], writes=[bk.b])
            P.op("act", lambda e, cb=cb, bk=bk: e.activation(out=vt.v(cb * 512, [[1, 512]]), in_=bk.v(), func=AF.Copy),
                 reads=[bk.b], writes=[vt.parts[cb]])
            yield
        for (Wx, half) in ((Wq, 0), (Wk, 1)):
            bk = C.bank("A")
            for h in range(4):
                for kc in range(8):
                    P.op("pe", lambda e, kc=kc, h=h, bk=bk, Wx=Wx: e.matmul(
                        bk.v(h * 128, [[1, 128]]), lhsT=Wx.v(kc * 512 + h * 128, [[1, 128]]),
                        rhs=hT.v(kc * 128, [[1, 128]]), start=(kc == 0), stop=(kc == 7)),
                        reads=hTr + [Wx.b], writes=[bk.b])
                if h == 1:
                    yield
            P.op("dve", lambda e, bk=bk, half=half: e.tensor_copy(out=qk.v(half * 512, [[1, 512]]), in_=bk.v()),
                 reads=[bk.b], writes=[qk.parts[half]])
            yield
        br = C.bank("A")
        for kc in range(8):
            P.op("pe", lambda e, kc=kc, br=br: e.matmul(
                br.v(0, [[1, 128]], 0, 16), lhsT=Wr.v(kc * 16, [[1, 16]]), rhs=hT.v(kc * 128, [[1, 128]]),
                start=(kc == 0), stop=(kc == 7)), reads=hTr + [Wr.b], writes=[br.b], small=True)
        P.op("dve", lambda e, br=br: e.tensor_copy(out=rTs.v(0, [[1, 128]], 0, 16), in_=br.v(0, [[1, 128]], 0, 16)),
             reads=[br.b], writes=[rTs.b])
        yield
        for h in range(4):
            bz = C.bank("A")
            for kc in range(8):
                P.op("pe", lambda e, kc=kc, h=h, bz=bz: e.matmul(
                    bz.v(), lhsT=hT.v(kc * 128, [[1, 128]]), rhs=Wz.v(kc * 2048 + h * 512, [[1, 512]]),
                    start=(kc == 0), stop=(kc == 7)), reads=hTr + [Wz.b], writes=[bz.b])
            P.op("act", lambda e, bz=bz, h=h: e.activation(out=sz.v(h * 512, [[1, 512]]), in_=bz.v(), func=AF.Silu),
                 reads=[bz.b], writes=[sz.parts[h]])
            P.op("pool", lambda e, h=h: e.tensor_tensor(out=sz.v(h * 512, [[1, 512]]), in0=sz.v(h * 512, [[1, 512]]),
                                                        in1=gain.v(), op=ALU.mult),
                 reads=[sz.parts[h], gain.b], writes=[sz.parts[h]])
            yield

    def stageB(t):
        xt, vt, sz, qk, rTs = xts[t % 2], vts[t % 2], szs[t % 2], qks[t % 2], rTss[t % 2]
        bp = C.bank("B")
        P.op("pe", lambda e, bp=bp: e.matmul(bp.v(), lhsT=rTs.v(0, [[1, 128]], 0, 16), rhs=Wg.v(0, [[1, 512]], 0, 16),
                                            start=True, stop=False), reads=[rTs.b, Wg.b], writes=[bp.b], small=True)
        P.op("pe", lambda e, bp=bp: e.matmul(bp.v(), lhsT=ones.v(0, [[1, 128]], 0, 1), rhs=bg.v(0, [[1, 512]], 0, 1),
                                            start=False, stop=True), reads=[ones.b, bg.b], writes=[bp.b], small=True)
        P.op("act", lambda e, bp=bp: e.activation(out=lsp.v(), in_=bp.v(), func=AF.Exp, scale=-1.0),
             reads=[bp.b], writes=[lsp.b])
        P.op("act", lambda e: e.activation(out=lsp.v(), in_=lsp.v(), func=AF.Ln, bias=1.0),
             reads=[lsp.b], writes=[lsp.b])
        yield
        bb = C.bank("B")
        for h in range(4):
            P.op("pe", lambda e, h=h, bb=bb: e.matmul(bb.v(h * 128, [[1, 128]]), lhsT=lsp.v(h * 128, [[1, 128]]),
                                                     rhs=tri.v(), start=True, stop=True),
                 reads=[lsp.b, tri.b], writes=[bb.b])
        P.op("act", lambda e, bb=bb: e.activation(out=E1.v(), in_=bb.v(), func=AF.Exp), reads=[bb.b], writes=[E1.b])
        P.op("act", lambda e, bb=bb: e.activation(out=E2.v(), in_=bb.v(), func=AF.Exp, scale=-1.0),
             reads=[bb.b], writes=[E2.b])
        yield
        P.op("dve", lambda e: e.scalar_tensor_tensor(out=qg.v(), in0=qk.v(0, [[1, 512]]), scalar=128.0 ** -0.5, in1=E1.v(),
                                                     op0=ALU.mult, op1=ALU.mult),
             reads=[qk.parts[0], E1.b], writes=[qg.b])
        P.op("dve", lambda e: e.tensor_tensor(out=kg.v(), in0=qk.v(512, [[1, 512]]), in1=E2.v(), op=ALU.mult),
             reads=[qk.parts[1], E2.b], writes=[kg.b])
        P.op("pool", lambda e: e.tensor_tensor(out=kendT.v(0, [[128, 4], [1, 128]]), in0=kg.v(0, [[128, 4], [1, 128]]),
                                               in1=E1.v(127, [[128, 4], [0, 128]]), op=ALU.mult),
             reads=[kg.b, E1.b], writes=[kendT.b])
        ba = C.bank("B")
        for h in range(4):
            P.op("pe", lambda e, h=h, ba=ba: e.matmul(ba.v(h * 128, [[1, 128]]), lhsT=kg.v(h * 128, [[1, 128]]),
                                                     rhs=qg.v(h * 128, [[1, 128]]), start=True, stop=True),
                 reads=[kg.b, qg.b], writes=[ba.b])
        P.op("dve", lambda e, ba=ba: e.tensor_tensor(out=att.v(0, [[128, 4], [1, 128]]), in0=ba.v(0, [[128, 4], [1, 128]]),
                                                    in1=mask01.v(0, [[0, 4], [1, 128]]), op=ALU.mult),
             reads=[ba.b, mask01.b], writes=[att.b])
        yield
        be = C.bank("B")
        for h in range(4):
            P.op("pe", lambda e, h=h, be=be: e.transpose(out=_bf_ap(be, h * 128, [[1, 128]], 0, 128),
                                                        in_=kendT.v(h * 128, [[1, 128]]), identity=identb.v()),
                 reads=[kendT.b, identb.b], writes=[be.b])
        P.op("act", lambda e, be=be: e.activation(out=kend.v(), in_=_bf_ap(be, 0, [[1, 512]], 0, 128), func=AF.Copy),
             reads=[be.b], writes=[kend.b])
        yield
        for h in range(4):
            bo = C.bank("O")
            P.op("pe", lambda e, h=h, bo=bo: e.matmul(bo.v(), lhsT=att.v(h * 128, [[1, 128]]), rhs=vt.v(h * 512, [[1, 512]]),
                                                     start=True, stop=False), reads=[att.b, vt.parts[h]], writes=[bo.b])
            P.op("pe", lambda e, h=h, bo=bo: e.matmul(bo.v(), lhsT=qg.v(h * 128, [[1, 128]]), rhs=Sbf.v(h * 512, [[1, 512]]),
                                                     start=False, stop=True), reads=[qg.b, Sbf.parts[h]], writes=[bo.b])
            P.op("act", lambda e, h=h, bo=bo: e.activation(out=junk.v(), in_=bo.v(), func=AF.Square,
                                                          accum_out=ssqo.v(h, [[1, 1]])),
                 reads=[bo.b], writes=[junk.b, ssqo.parts[h]])
            P.op("dve", lambda e, h=h: e.tensor_scalar(out=varo.v(h, [[1, 1]]), in0=ssqo.v(h, [[1, 1]]), scalar1=1.0 / 512,
                                                       scalar2=EPS, op0=ALU.mult, op1=ALU.add),
                 reads=[ssqo.parts[h]], writes=[varo.parts[h]])
            P.op("pool", lambda e, h=h: e.tensor_tensor(out=rstdo.v(h, [[1, 1]]), in0=varo.v(h, [[1, 1]]),
                                                        in1=nh.v(0, [[1, 1]]), op=ALU.pow),
                 reads=[varo.parts[h], nh.b], writes=[rstdo.parts[h]])
            yield
            P.op("dve", lambda e, h=h, bo=bo: e.scalar_tensor_tensor(
                out=og.v(h * 512, [[1, 512]]), in0=bo.v(), scalar=rstdo.v(h, [[1, 1]]), in1=sz.v(h * 512, [[1, 512]]),
                op0=ALU.mult, op1=ALU.mult), reads=[bo.b, rstdo.parts[h], sz.parts[h]], writes=[og.parts[h]])
            bs = C.bank("O")
            P.op("pe", lambda e, h=h, bs=bs: e.matmul(bs.v(), lhsT=kend.v(h * 128, [[1, 128]]), rhs=vt.v(h * 512, [[1, 512]]),
                                                     start=True, stop=True), reads=[kend.b, vt.parts[h]], writes=[bs.b])
            P.op("dve", lambda e, h=h, bs=bs: e.scalar_tensor_tensor(
                out=S.v(h * 512, [[1, 512]]), in0=S.v(h * 512, [[1, 512]]), scalar=E1.v(h * 128 + 127, [[1, 1]]),
                in1=bs.v(), op0=ALU.mult, op1=ALU.add), reads=[S.parts[h], E1.b, bs.b], writes=[S.parts[h]])
            P.op("act", lambda e, h=h: e.activation(out=Sbf.v(h * 512, [[1, 512]]), in_=S.v(h * 512, [[1, 512]]), func=AF.Copy),
                 reads=[S.parts[h]], writes=[Sbf.parts[h]])
            yield
        for q4 in range(4):
            bt = C.bank("C")
            for q in range(4):
                ec = q4 * 4 + q
                P.op("pe", lambda e, ec=ec, q=q, bt=bt: e.transpose(out=_bf_ap(bt, q * 128, [[1, 128]], 0, 128),
                                                                   in_=og.v(ec * 128, [[1, 128]]), identity=identb.v()),
                     reads=[og.parts[ec // 4], identb.b], writes=[bt.b])
            if q4 % 2 == 0:
                P.op("act", lambda e, q4=q4, bt=bt: e.activation(out=ogT.v(q4 * 512, [[1, 512]]),
                                                                in_=_bf_ap(bt, 0, [[1, 512]], 0, 128), func=AF.Copy),
                     reads=[bt.b], writes=[ogT.parts[q4]])
            else:
                P.op("dve", lambda e, q4=q4, bt=bt: e.tensor_copy(out=ogT.v(q4 * 512, [[1, 512]]),
                                                                 in_=_bf_ap(bt, 0, [[1, 512]], 0, 128)),
                     reads=[bt.b], writes=[ogT.parts[q4]])
            yield
        for nb in range(2):
            bo2 = C.bank("C")
            for ec in range(16):
                P.op("pe", lambda e, ec=ec, nb=nb, bo2=bo2: e.matmul(
                    bo2.v(), lhsT=ogT.v(ec * 128, [[1, 128]]), rhs=Wo.v(ec * 1024 + nb * 512, [[1, 512]]),
                    start=(ec == 0), stop=(ec == 15)), reads=[ogT.parts[ec // 4], Wo.b], writes=[bo2.b])
                if ec == 7:
                    yield
            P.op("dve", lambda e, nb=nb, bo2=bo2: e.tensor_tensor(
                out=xo.v(nb * 512, [[1, 512]]), in0=bo2.v(), in1=xt.v(nb * 512, [[1, 512]]), op=ALU.add),
                reads=[bo2.b, xt.b], writes=[xo.b])
            yield
        lv = dict(ssq=ssq2, var=var2, sd=None, rstd=rstd2, fng=fng, yo=yo) if final is not None else {}
        emit_out(C, t, xo, x_dst, final, lv)
        yield

    for _ in stageA(0):
        pass
    for t in range(NT):
        streams = [stageB(t)]
        if t + 1 < NT:
            streams.append(stageA(t + 1))
        interleave(streams)
    P.barrier()
    P.flush(C.block)
    es.close()


def emit_out(C, t, xo, x_dst, final, lv, row_ap=None):
    P = C.P
    if row_ap is None:
        row_ap = bass.AP(x_dst, t * 128 * 1024, [[1024, 128], [1, 1024]])
    if final is None:
        P.op("sp", lambda e: e.dma_start(out=row_ap, in_=xo.v()), reads=[xo.b], dma=True)
    else:
        fng, yo = lv["fng"], lv["yo"]
        ssq, var, sd, rstd = lv["ssq"], lv["var"], lv["sd"], lv["rstd"]
        rms_prep(C, xo, yo, ssq, var, sd, rstd, D)
        P.op("pool", lambda e: e.tensor_tensor(out=yo.v(), in0=yo.v(), in1=fng.v(), op=ALU.mult),
             reads=[yo.b, fng.b], writes=[yo.b])
        P.op("sp", lambda e: e.dma_start(out=row_ap, in_=yo.v()), reads=[yo.b], dma=True)


def host_consts():
    s = np.arange(128)
    tri = np.where(s[:, None] <= s[None, :], -1.0 / 16.0, 0.0).astype(np.float32)
    mask = np.where(s[:, None] <= s[None, :], 1.0, 0.0).astype(np.float32)
    return {
        "c_ident": np.eye(128, dtype=np.float32),
        "c_tri": tri,
        "c_mask": mask,
        "c_ones": np.ones((128, 128), dtype=np.float32),
    }


GLA_W = {"w_in": [1024, GLA_IN], "w_out": [2048, 1024], "w_gate_up": [16, 512], "b_gate": [1, 512],
         "norm_col": [128, 8], "gain_rep": [128, 512]}


DEBUG = False
DBG_NAMES = []


def build_program(layers, final):
    nc = bass.Bass("TRN2", target_bir_lowering=False, dynamic_dma_scratch_size=4096)
    x_in = nc.dram_tensor("x", [L, D], F32, kind="ExternalInput")
    y_out = nc.dram_tensor("y", [L, D], F32, kind="ExternalOutput")
    cd = {k: nc.dram_tensor(k, [128, 128], F32, kind="ExternalInput") for k in ("c_ident", "c_tri", "c_mask", "c_ones")}
    fn_d = nc.dram_tensor("final_rep", [128, 1024], F32, kind="ExternalInput") if final else None
    Wd = []
    for li, (kind, j) in enumerate(layers):
        spec = GLA_W if kind == "gla" else S5_W
        Wd.append({k: nc.dram_tensor("L%d_%s" % (li, k), sh, F32, kind="ExternalInput") for k, sh in spec.items()})
    scr = [nc.dram_tensor("xscr%d" % i, [L, D], F32, kind="Internal") for i in range(2)] if len(layers) > 1 else []
    with contextlib.ExitStack() as es:
        P = Prog(nc, es)
        C = Ctx(nc, es, P)
        C.dbg = DEBUG
        C.dbg_names = DBG_NAMES
        for i in range(8):
            h = es.enter_context(nc.psum_tensor("pb%d" % i, [128, 512], F32))
            C.banks.append(TT(h, 512, "pb%d" % i))
        consts = {}
        for k, nm in (("ident", "c_ident"), ("tri", "c_tri"), ("mask", "c_mask"), ("ones", "c_ones")):
            tt = C.sb(es, k, 128, F32)
            consts[k] = tt
            P.op("sp", lambda e, tt=tt, nm=nm: e.dma_start(out=tt.v(), in_=bass.AP(cd[nm], 0, [[128, 128], [1, 128]])),
                 writes=[tt.b], dma=True)
        neghalf = C.sb(es, "neghalf", 4, F32)
        consts["neghalf"] = neghalf
        P.op("pool", lambda e: e.memset(neghalf.v(), -0.5), writes=[neghalf.b])
        C.consts = consts
        identb = C.sb(es, "identb", 128, BF16)
        consts["identb"] = identb
        P.op("dve", lambda e: e.tensor_copy(out=identb.v(), in_=consts["ident"].v()), reads=[consts["ident"].b],
             writes=[identb.b])
        block = es.enter_context(nc.Block())
        C.block = block
        for li, (kind, j) in enumerate(layers):
            src = x_in if li == 0 else scr[(li - 1) % 2]
            last = li == len(layers) - 1
            dst = y_out if last else scr[li % 2]
            fin = fn_d if (last and final) else None
            if kind == "gla":
                gla_layer(C, es, src, dst, Wd[li], consts, fin)
            else:
                s5_layer(C, es, src, dst, Wd[li], consts, fin)
        P.flush(block, finish=True)
    return nc


S5_W = {"w_in": [1024, 4096], "w_glu": [2048, 2048], "w_out": [2048, 1024], "b_glu": [1, 2048],
        "norm_col": [128, 8], "lam_re": [128, 64], "lam_im": [128, 64], "logdt": [128, 64],
        "b_re": [128, 1024], "b_im": [128, 1024], "c_re": [128, 1024], "c_im": [128, 1024],
        "dcol": [128, 128], "kE": [128, 8], "kQ": [128, 8], "kA": [128, 16], "kB": [128, 16], "maskM": [128, 128]}

TWO_PI = 2.0 * math.pi
MAGIC = 12582912.0


def s5_layer(C, es0, x_src, x_dst, W, consts, final=None):
    nc, P = C.nc, C.P
    ident, identb, ones = consts["ident"], consts["identb"], consts["ones"]
    esL = es0.enter_context(contextlib.ExitStack())
    Utm = C.sb(esL, "Utm", 16 * 2048, BF16, nparts=16)
    ucols = [Buf("ucol%d" % q) for q in range(64)]
    gcol = C.sb(esL, "gcol5", 8, F32)
    ssq = C.sb(esL, "ssq5", 1, F32)
    var = C.sb(esL, "var5", 1, F32)
    sd = C.sb(esL, "sd5", 1, F32)
    rstd = C.sb(esL, "rstd5", 1, F32)
    P.op("sp", lambda e: e.dma_start(out=gcol.v(), in_=bass.AP(W["norm_col"], 0, [[8, 128], [1, 8]])),
         writes=[gcol.b], dma=True)

    def xrow_ap(h, tt):
        blk, tau = tt // 8, tt % 8
        return bass.AP(h, (blk * 1024 + tau) * 1024, [[8 * 1024, 128], [1, 1024]])

    es = esL.enter_context(contextlib.ExitStack())
    sb = lambda name, n, dt, nparts=1: C.sb(es, name, n, dt, nparts)
    Wu = sb("Wu", 8 * 2048, BF16)
    xts = [sb("xtA%d" % i, 1024, F32) for i in range(2)]
    xn = sb("xnA", 1024, F32)
    hT = sb("hTA", 1024, BF16, nparts=2)
    win = W["w_in"]
    wap = lambda r0, c0, cn: bass.AP(win, r0 * 4096 + c0, [[4096, 128], [1, cn]])
    load_w_cast(C, Wu, wap, 8, 2048, 0, None, None)
    for tt in range(16):
        xt = xts[tt % 2]
        P.op("sp", lambda e, tt=tt, xt=xt: e.dma_start(out=xt.v(), in_=xrow_ap(x_src, tt)), writes=[xt.b], dma=True)
        rms_prep(C, xt, xn, ssq, var, sd, rstd, D)
        make_hT(C, xn, hT, gcol, ident)
        for cb in range(4):
            bk = C.bank()
            for kc in range(8):
                P.op("pe", lambda e, kc=kc, cb=cb, bk=bk: e.matmul(
                    bk.v(), lhsT=hT.v(kc * 128, [[1, 128]]), rhs=Wu.v(kc * 2048 + cb * 512, [[1, 512]]),
                    start=(kc == 0), stop=(kc == 7)), reads=hT.parts + [Wu.b], writes=[bk.b])
            if cb % 2 == 0:
                P.op("act", lambda e, tt=tt, cb=cb, bk=bk: e.activation(
                    out=Utm.v(tt * 2048 + cb * 512, [[1, 512]]), in_=bk.v(), func=AF.Copy),
                    reads=[bk.b], writes=[Utm.parts[tt]])
            else:
                P.op("dve", lambda e, tt=tt, cb=cb, bk=bk: e.tensor_copy(
                    out=Utm.v(tt * 2048 + cb * 512, [[1, 512]]), in_=bk.v()),
                    reads=[bk.b], writes=[Utm.parts[tt]])
    P.barrier()
    P.flush(C.block)
    es.close()

    es = esL.enter_context(contextlib.ExitStack())
    sb = lambda name, n, dt, nparts=1: C.sb(es, name, n, dt, nparts)
    PERSIST = ("c_re", "c_im", "dcol", "maskM")
    pre = {}
    for nm, n in (("Bre", 1024), ("Bim", 1024), ("Esn", 512), ("Ecs", 512), ("Qsn", 512), ("Qcs", 512), ("rho8", 64),
                  ("RAsn", 1024), ("RAcs", 1024), ("RBsn", 1024), ("RBcs", 1024)):
        pre[nm] = C.sb(es, "s5t_" + nm, n, F32)
    for nm, n in (("c_re", 1024), ("c_im", 1024), ("dcol", 128), ("maskM", 128)):
        pre["ld_" + nm] = C.sb(es, "s5_" + nm, n, F32)
    esT = es.enter_context(contextlib.ExitStack())
    ld = {}
    for nm, n in (("lam_re", 64), ("lam_im", 64), ("logdt", 64), ("b_re", 1024), ("b_im", 1024), ("c_re", 1024),
                  ("c_im", 1024), ("dcol", 128), ("kE", 8), ("kQ", 8), ("kA", 16), ("kB", 16), ("maskM", 128)):
        tt_ = pre["ld_" + nm] if nm in PERSIST else C.sb(esT, "s5_" + nm, n, F32)
        ld[nm] = tt_
        P.op("sp", lambda e, tt_=tt_, nm=nm, n=n: e.dma_start(out=tt_.v(), in_=bass.AP(W[nm], 0, [[n, 128], [1, n]])),
             writes=[tt_.b], dma=True)
    lam_re, lam_im = ld["lam_re"], ld["lam_im"]

    def T(name, n, dt=F32, keep=True):
        if name in pre:
            return pre[name]
        assert (not keep) or esT_closed[0], name
        return C.sb(es if keep else esT, "s5t_" + name, n, dt)

    esT_closed = [False]

    def tsc(out, in0, s1, s2, op0, op1=None, eng="dve", rd=(), n=None):
        kw = dict(out=out[0].v(*out[1:]), in0=in0[0].v(*in0[1:]), scalar1=s1, scalar2=s2, op0=op0)
        if op1 is not None:
            kw["op1"] = op1
        P.op(eng, lambda e: e.tensor_scalar(**kw), reads=[in0[0].b] + list(rd), writes=[out[0].b])

    def tt2(out, a, b, op, eng="dve"):
        P.op(eng, lambda e: e.tensor_tensor(out=out[0].v(*out[1:]), in0=a[0].v(*a[1:]), in1=b[0].v(*b[1:]), op=op),
             reads=[a[0].b, b[0].b], writes=[out[0].b])

    def actf(out, in_, func, scale=None, bias=None):
        kw = {}
        if scale is not None:
            kw["scale"] = scale
        if bias is not None:
            kw["bias"] = bias
        P.op("act", lambda e: e.activation(out=out[0].v(*out[1:]), in_=in_[0].v(*in_[1:]), func=func, **kw),
             reads=[in_[0].b], writes=[out[0].b])

    def range_reduce(dst, src, n, tmp):
        tsc((tmp,), (src,), 1.0 / TWO_PI, MAGIC, ALU.mult, ALU.add)
        tsc((tmp,), (tmp,), -MAGIC, None, ALU.add)
        P.op("dve", lambda e: e.scalar_tensor_tensor(out=dst.v(), in0=tmp.v(), scalar=-TWO_PI, in1=src.v(),
                                                     op0=ALU.mult, op1=ALU.add),
             reads=[tmp.b, src.b], writes=[dst.b])
        tsc((dst,), (dst,), math.pi, -math.pi, ALU.min, ALU.max)

    def sincos(sin_t, cos_t, ang, n, tmp, tmp2):
        range_reduce(tmp2, ang, n, tmp)
        actf((sin_t,), (tmp2,), AF.Sin)
        tsc((tmp2,), (ang,), math.pi / 2, None, ALU.add)
        range_reduce(tmp2, tmp2, n, tmp)
        actf((cos_t,), (tmp2,), AF.Sin)

    dt = T("dt", 64, keep=False)
    a = T("a", 64, keep=False)
    th = T("th", 64, keep=False)
    actf((dt,), (ld["logdt"],), AF.Exp)
    tt2((a,), (lam_re,), (dt,), ALU.mult)
    tt2((th,), (lam_im,), (dt,), ALU.mult)
    mag = T("mag", 64, keep=False)
    c1 = T("c1", 64, keep=False)
    s1 = T("s1", 64, keep=False)
    tA = T("tA", 64, keep=False)
    tB = T("tB", 64, keep=False)
    actf((mag,), (a,), AF.Exp)
    sincos(s1, c1, th, 64, tA, tB)
    lbr = T("lbr", 64, keep=False)
    lbi = T("lbi", 64, keep=False)
    tt2((lbr,), (mag,), (c1,), ALU.mult)
    tt2((lbi,), (mag,), (s1,), ALU.mult)
    tsc((lbr,), (lbr,), -1.0, None, ALU.add)
    nr = T("nr", 64, keep=False)
    ni = T("ni", 64, keep=False)
    den = T("den", 64, keep=False)
    tt2((nr,), (lbr,), (lam_re,), ALU.mult)
    tt2((tA,), (lbi,), (lam_im,), ALU.mult)
    tt2((nr,), (nr,), (tA,), ALU.add)
    tt2((ni,), (lbi,), (lam_re,), ALU.mult)
    tt2((tA,), (lbr,), (lam_im,), ALU.mult)
    tt2((ni,), (ni,), (tA,), ALU.subtract)
    tt2((den,), (lam_re,), (lam_re,), ALU.mult)
    tt2((tA,), (lam_im,), (lam_im,), ALU.mult)
    tt2((den,), (den,), (tA,), ALU.add)
    P.op("dve", lambda e: e.reciprocal(out=den.v(), in_=den.v()), reads=[den.b], writes=[den.b])
    tt2((nr,), (nr,), (den,), ALU.mult)
    tt2((ni,), (ni,), (den,), ALU.mult)
    Bre = T("Bre", 1024)
    Bim = T("Bim", 1024)
    t1k = T("t1k", 1024, keep=False)
    bc_b = lambda t_: (t_, 0, [[1, 64], [0, 16]])
    full3 = lambda t_: (t_, 0, [[16, 64], [1, 16]])
    tt2(full3(Bre), bc_b(nr), full3(ld["b_re"]), ALU.mult)
    tt2(full3(t1k), bc_b(ni), full3(ld["b_im"]), ALU.mult)
    tt2((Bre,), (Bre,), (t1k,), ALU.subtract)
    tt2(full3(Bim), bc_b(nr), full3(ld["b_im"]), ALU.mult)
    tt2(full3(t1k), bc_b(ni), full3(ld["b_re"]), ALU.mult)
    tt2((Bim,), (Bim,), (t1k,), ALU.add)

    def powers(kname, nk, tag):
        kv = ld[kname]
        ka = T(tag + "ka", 64 * nk, keep=False)
        kt = T(tag + "kt", 64 * nk, keep=False)
        m_ = T(tag + "m", 64 * nk, keep=False)
        sn = T(tag + "sn", 64 * nk)
        cs = T(tag + "cs", 64 * nk)
        w1 = T(tag + "w1", 64 * nk, keep=False)
        w2 = T(tag + "w2", 64 * nk, keep=False)
        o3 = lambda t_: (t_, 0, [[nk, 64], [1, nk]])
        tt2(o3(ka), (a, 0, [[1, 64], [0, nk]]), (kv, 0, [[0, 64], [1, nk]]), ALU.mult)
        tt2(o3(kt), (th, 0, [[1, 64], [0, nk]]), (kv, 0, [[0, 64], [1, nk]]), ALU.mult)
        actf((m_,), (ka,), AF.Exp)
        sincos(sn, cs, kt, 64 * nk, w1, w2)
        tt2((cs,), (cs,), (m_,), ALU.mult)
        tt2((sn,), (sn,), (m_,), ALU.mult)
        return cs, sn

    pEr, pEi = powers("kE", 8, "E")
    pQr, pQi = powers("kQ", 8, "Q")
    rho8 = T("rho8", 64)
    actf((rho8,), (a,), AF.Exp, scale=8.0)
    th8 = T("th8", 64, keep=False)
    tsc((th8,), (th,), 8.0, None, ALU.mult)
    thr = T("thr", 64, keep=False)
    range_reduce(thr, th8, 64, tA)

    def rot_tables(kname, tag):
        kv = ld[kname]
        ang = T(tag + "ang", 1024, keep=False)
        sn = T(tag + "sn", 1024)
        cs = T(tag + "cs", 1024)
        w1 = T(tag + "w1", 1024, keep=False)
        w2 = T(tag + "w2", 1024, keep=False)
        tt2((ang, 0, [[16, 64], [1, 16]]), (thr, 0, [[1, 64], [0, 16]]), (kv, 0, [[0, 64], [1, 16]]), ALU.mult)
        sincos(sn, cs, ang, 1024, w1, w2)
        return cs, sn

    CA, SA = rot_tables("kA", "RA")
    CB, SB = rot_tables("kB", "RB")

    P.barrier()
    P.flush(C.block)
    esT.close()
    esT_closed[0] = True

    NB = 3
    EB = [[T("EB%d_%d" % (c, i), 128, BF16) for c in range(2)] for i in range(NB)]
    QC = [[T("QC%d_%d" % (c, i), 128, BF16) for c in range(2)] for i in range(NB)]
    tmp4 = [[T("tm%d_%d" % (c, i), 128) for c in range(4)] for i in range(2)]
    Wop = [T("Wop%d" % i, 256, BF16) for i in range(NB)]
    Mtmp = [T("Mtmp%d" % i, 256) for i in range(2)]
    Mop = [T("Mop%d" % i, 256, BF16) for i in range(NB)]
    Upair = [T("Upair%d" % i, 512, BF16) for i in range(NB)]
    Ustg = [C.sb(es, "Ustg%d" % i, 512, BF16, nparts=2) for i in range(2)]
    cosN = [T("cosN%d" % i, 256) for i in range(2)]
    sinN = [T("sinN%d" % i, 256) for i in range(2)]
    rt = [[T("rt%d_%d" % (c, i), 256) for c in range(2)] for i in range(2)]
    Pt = [[T("Pt%d_%d" % (c, i), 256) for c in range(2)] for i in range(2)]
    St = [[T("St%d_%d" % (c, i), 256) for c in range(2)] for i in range(2)]
    mt = [[T("mt%d_%d" % (c, i), 256) for c in range(2)] for i in range(2)]
    Sp = [T("Sp%d" % i, 512, BF16) for i in range(NB)]
    Ysb = [T("Ysb%d" % i, 512, BF16) for i in range(2)]
    pairbank = {}

    def outer(out_t, A, offA, B, offB, eng):
        P.op(eng, lambda e: e.tensor_tensor(out=out_t.v(0, [[16, 8], [1, 16]]), in0=A.v(offA, [[1, 8], [0, 16]]),
                                            in1=B.v(offB, [[0, 8], [1, 16]]), op=ALU.mult),
             reads=[A.b, B.b], writes=[out_t.b])

    def stage1(q):
        i3, i2 = q % NB, q % 2
        t0, t1, t2, t3 = tmp4[i2]
        outer(t0, pEr, q * 8, Bre, q * 16, "dve")
        outer(t1, pEi, q * 8, Bim, q * 16, "pool")
        outer(t2, pEr, q * 8, Bim, q * 16, "dve")
        outer(t3, pEi, q * 8, Bre, q * 16, "pool")
        tt2((EB[i3][0],), (t0,), (t1,), ALU.subtract)
        tt2((EB[i3][1],), (t2,), (t3,), ALU.add, eng="pool")
        cre, cim = ld["c_re"], ld["c_im"]
        outer(t0, pQr, q * 8, cre, q * 16, "dve")
        outer(t1, pQi, q * 8, cim, q * 16, "pool")
        outer(t2, pQr, q * 8, cim, q * 16, "dve")
        outer(t3, pQi, q * 8, cre, q * 16, "pool")
        tt2((QC[i3][0],), (t0,), (t1,), ALU.subtract)
        P.op("dve", lambda e: e.scalar_tensor_tensor(out=QC[i3][1].v(), in0=t2.v(), scalar=-1.0, in1=t3.v(),
                                                     op0=ALU.mult, op1=ALU.subtract),
             reads=[t2.b, t3.b], writes=[QC[i3][1].b])
        bw = C.bank()
        for g2 in range(2):
            for c in range(2):
                P.op("pe", lambda e, g2=g2, c=c, bw=bw: e.transpose(
                    out=_bf_ap(bw, (g2 * 2 + c) * 64, [[1, 64]], 0, 128),
                    in_=EB[i3][c].v(0, [[1, 128]], g2 * 64, 64),
                    identity=identb.v(g2 * 64, [[1, 64]], g2 * 64, 64)),
                    reads=[EB[i3][c].b, identb.b], writes=[bw.b], small=True)
        P.op("act", lambda e, bw=bw: e.activation(out=Wop[i3].v(), in_=_bf_ap(bw, 0, [[1, 256]], 0, 128), func=AF.Copy),
             reads=[bw.b], writes=[Wop[i3].b])
        bm = C.bank()
        for g2 in range(2):
            for c in range(2):
                P.op("pe", lambda e, g2=g2, c=c, bm=bm: e.matmul(
                    bm.v(g2 * 128, [[1, 128]]), lhsT=EB[i3][c].v(0, [[1, 128]], g2 * 64, 64),
                    rhs=QC[i3][c].v(0, [[1, 128]], g2 * 64, 64), start=(c == 0), stop=(c == 1)),
                    reads=[EB[i3][c].b, QC[i3][c].b], writes=[bm.b], small=True)
        P.op("dve", lambda e, bm=bm: e.tensor_tensor(out=Mtmp[i2].v(0, [[128, 2], [1, 128]]), in0=bm.v(0, [[128, 2], [1, 128]]),
                                                    in1=ld["maskM"].v(0, [[0, 2], [1, 128]]), op=ALU.mult),
             reads=[bm.b, ld["maskM"].b], writes=[Mtmp[i2].b])
        for g2 in range(2):
            g = 2 * q + g2
            P.op("dve", lambda e, g2=g2, g=g: e.scalar_tensor_tensor(
                out=Mop[i3].v(g2 * 128, [[1, 128]]), in0=ident.v(), scalar=ld["dcol"].v(g, [[1, 1]]),
                in1=Mtmp[i2].v(g2 * 128, [[1, 128]]), op0=ALU.mult, op1=ALU.add),
                reads=[ident.b, ld["dcol"].b, Mtmp[i2].b], writes=[Mop[i3].b])
        bu = C.bank()
        ust = Ustg[i2]
        for blk in range(2):
            if blk == 0:
                P.op("act", lambda e, blk=blk: e.activation(
                    out=ust.v(blk * 256, [[128, 2], [16, 8], [1, 16]]),
                    in_=Utm.v(blk * 8 * 2048 + 32 * q, [[16, 2], [2048, 8], [1, 16]]), func=AF.Copy),
                    reads=[ucols[q]], writes=[ust.parts[blk]])
            else:
                P.op("pool", lambda e, blk=blk: e.tensor_copy(
                    out=ust.v(blk * 256, [[128, 2], [16, 8], [1, 16]]),
                    in_=Utm.v(blk * 8 * 2048 + 32 * q, [[16, 2], [2048, 8], [1, 16]])),
                    reads=[ucols[q]], writes=[ust.parts[blk]])
        for g2 in range(2):
            for blk in range(2):
                P.op("pe", lambda e, g2=g2, blk=blk, bu=bu: e.transpose(
                    out=_bf_ap(bu, (g2 * 2 + blk) * 128, [[1, 128]], 0, 128),
                    in_=ust.v(blk * 256 + g2 * 128, [[1, 128]]), identity=identb.v()),
                    reads=[ust.parts[blk], identb.b], writes=[bu.b])
        P.op("act", lambda e, bu=bu: e.activation(out=Upair[i3].v(), in_=_bf_ap(bu, 0, [[1, 512]], 0, 128), func=AF.Copy),
             reads=[bu.b], writes=[Upair[i3].b])

    def stage2(q):
        i3, i2 = q % NB, q % 2
        bp = C.bank()
        pairbank[q] = bp
        for g2 in range(2):
            for c in range(2):
                P.op("pe", lambda e, g2=g2, c=c, bp=bp: e.matmul(
                    bp.v(c * 256, [[1, 256]], g2 * 64, 64), lhsT=Wop[i3].v((g2 * 2 + c) * 64, [[1, 64]]),
                    rhs=Upair[i3].v(g2 * 256, [[1, 256]]), start=True, stop=True),
                    reads=[Wop[i3].b, Upair[i3].b], writes=[bp.b], small=True)
        cN, sN = cosN[i2], sinN[i2]
        r0, r1 = rt[i2]

        def ab(out_t, A, B, eng):
            P.op(eng, lambda e: e.tensor_tensor(out=out_t.v(0, [[16, 16], [1, 16]]), in0=A.v(q * 16, [[1, 16], [0, 16]]),
                                                in1=B.v(q * 16, [[0, 16], [1, 16]]), op=ALU.mult),
                 reads=[A.b, B.b], writes=[out_t.b])
        ab(r0, CA, CB, "pool")
        ab(r1, SA, SB, "pool")
        tt2((cN,), (r0,), (r1,), ALU.subtract, eng="pool")
        ab(r0, SA, CB, "pool")
        ab(r1, CA, SB, "pool")
        tt2((sN,), (r0,), (r1,), ALU.add, eng="pool")
        m0, m1 = mt[i2]
        pre_ = (bp, 0, [[1, 256]])
        pim_ = (bp, 256, [[1, 256]])
        Ptr, Pti = Pt[i2]
        tt2((m0,), pre_, (cN,), ALU.mult)
        tt2((m1,), pim_, (sN,), ALU.mult)
        tt2((Ptr,), (m0,), (m1,), ALU.add)
        tt2((m0,), pim_, (cN,), ALU.mult)
        tt2((m1,), pre_, (sN,), ALU.mult)
        tt2((Pti,), (m0,), (m1,), ALU.subtract)
        Str, Sti = St[i2]
        for (So, Pi) in ((Str, Ptr), (Sti, Pti)):
            P.op("dve", lambda e, So=So, Pi=Pi: e.tensor_tensor_scan(
                out=So.v(), data0=rho8.v(q, [[0, 256]]), data1=Pi.v(), initial=0.0, op0=ALU.mult, op1=ALU.add),
                reads=[rho8.b, Pi.b], writes=[So.b])
        tt2((Str,), (Str,), (Ptr,), ALU.subtract, eng="pool")
        tt2((Sti,), (Sti,), (Pti,), ALU.subtract, eng="pool")
        tt2((r0,), (Str,), (cN,), ALU.mult, eng="pool")
        tt2((r1,), (Sti,), (sN,), ALU.mult, eng="pool")
        tt2((Sp[i3], 0, [[1, 256]]), (r0,), (r1,), ALU.subtract, eng="pool")
        tt2((r0,), (Str,), (sN,), ALU.mult, eng="pool")
        tt2((r1,), (Sti,), (cN,), ALU.mult, eng="pool")
        tt2((Sp[i3], 256, [[1, 256]]), (r0,), (r1,), ALU.add, eng="pool")

    def stage3(q):
        i3, i2 = q % NB, q % 2
        by = C.bank()
        for g2 in range(2):
            P.op("pe", lambda e, g2=g2, by=by: e.matmul(
                by.v(g2 * 256, [[1, 256]]), lhsT=Mop[i3].v(g2 * 128, [[1, 128]]), rhs=Upair[i3].v(g2 * 256, [[1, 256]]),
                start=True, stop=False), reads=[Mop[i3].b, Upair[i3].b], writes=[by.b])
            for c in range(2):
                P.op("pe", lambda e, g2=g2, c=c, by=by: e.matmul(
                    by.v(g2 * 256, [[1, 256]]), lhsT=QC[i3][c].v(0, [[1, 128]], g2 * 64, 64),
                    rhs=Sp[i3].v(c * 256, [[1, 256]], g2 * 64, 64), start=False, stop=(c == 1)),
                    reads=[QC[i3][c].b, Sp[i3].b], writes=[by.b], small=True)
        P.op("act", lambda e, by=by: e.activation(out=Ysb[i2].v(), in_=by.v(), func=AF.Gelu_apprx_tanh),
             reads=[by.b], writes=[Ysb[i2].b])
        bt = C.bank()
        for g2 in range(2):
            for blk in range(2):
                P.op("pe", lambda e, g2=g2, blk=blk, bt=bt: e.transpose(
                    out=_bf_ap(bt, (g2 * 2 + blk) * 128, [[1, 128]], 0, 128),
                    in_=Ysb[i2].v(g2 * 256 + blk * 128, [[1, 128]]), identity=identb.v()),
                    reads=[Ysb[i2].b, identb.b], writes=[bt.b])
        for g2 in range(2):
            g = 2 * q + g2
            eng = "act" if g2 == 0 else "dve"
            if eng == "act":
                P.op("act", lambda e, g=g, g2=g2, bt=bt: e.activation(
                    out=Utm.v(16 * g, [[8 * 2048, 2], [2048, 8], [1, 16]]),
                    in_=_bf_ap(bt, g2 * 256, [[128, 2], [16, 8], [1, 16]], 0, 128), func=AF.Copy),
                    reads=[bt.b], writes=[ucols[q]])
            else:
                P.op("dve", lambda e, g=g, g2=g2, bt=bt: e.tensor_copy(
                    out=Utm.v(16 * g, [[8 * 2048, 2], [2048, 8], [1, 16]]),
                    in_=_bf_ap(bt, g2 * 256, [[128, 2], [16, 8], [1, 16]], 0, 128)),
                    reads=[bt.b], writes=[ucols[q]])

    for step in range(64 + 2):
        if step < 64:
            stage1(step)
        if 0 <= step - 1 < 64:
            stage2(step - 1)
        if 0 <= step - 2 < 64:
            stage3(step - 2)
    P.barrier()
    P.flush(C.block)
    es.close()

    es = esL.enter_context(contextlib.ExitStack())
    sb = lambda name, n, dt, nparts=1: C.sb(es, name, n, dt, nparts)
    Wglu = sb("Wglu", 16 * 2048, BF16)
    bglu = sb("bglu", 2048, F32)
    ygT = [sb("ygT%d" % i, 2048, BF16, nparts=2) for i in range(2)]
    sig = [sb("sig%d" % i, 512, F32) for i in range(2)]
    wg_ = W["w_glu"]
    wgap = lambda r0, c0, cn: bass.AP(wg_, r0 * 2048 + c0, [[2048, 128], [1, cn]])
    load_w_cast(C, Wglu, wgap, 16, 2048, 0, None, None)
    P.op("sp", lambda e: e.dma_start(out=bglu.v(0, [[1, 2048]], 0, 1), in_=bass.AP(W["b_glu"], 0, [[2048, 1], [1, 2048]])),
         writes=[bglu.b], dma=True)
    for tt in range(16):
        yT = ygT[tt % 2]
        for half in range(2):
            bt = C.bank()
            for k8 in range(8):
                kc = half * 8 + k8
                P.op("pe", lambda e, kc=kc, k8=k8, bt=bt, tt=tt: e.transpose(
                    out=_bf_ap(bt, k8 * 128, [[1, 128]], 0, 128), in_=Utm.v(tt * 2048 + kc * 128, [[1, 128]]),
                    identity=identb.v()), reads=[Utm.parts[tt], identb.b], writes=[bt.b])
            if half == 0:
                P.op("act", lambda e, bt=bt, yT=yT: e.activation(out=yT.v(0, [[1, 1024]]), in_=_bf_ap(bt, 0, [[1, 1024]], 0, 128),
                                                                func=AF.Copy), reads=[bt.b], writes=[yT.parts[0]])
            else:
                P.op("dve", lambda e, bt=bt, yT=yT: e.tensor_copy(out=yT.v(1024, [[1, 1024]]), in_=_bf_ap(bt, 0, [[1, 1024]], 0, 128)),
                     reads=[bt.b], writes=[yT.parts[1]])
        for cb in range(4):
            bk = C.bank()
            for kc in range(16):
                P.op("pe", lambda e, kc=kc, cb=cb, bk=bk, yT=yT: e.matmul(
                    bk.v(), lhsT=yT.v(kc * 128, [[1, 128]]), rhs=Wglu.v(kc * 2048 + cb * 512, [[1, 512]]),
                    start=(kc == 0), stop=False), reads=[yT.parts[kc // 8], Wglu.b], writes=[bk.b])
            P.op("pe", lambda e, cb=cb, bk=bk: e.matmul(
                bk.v(), lhsT=ones.v(0, [[1, 128]], 0, 1), rhs=bglu.v(cb * 512, [[1, 512]], 0, 1), start=False, stop=True),
                reads=[ones.b, bglu.b], writes=[bk.b], small=True)
            sg = sig[cb % 2]
            P.op("act", lambda e, bk=bk, sg=sg: e.activation(out=sg.v(), in_=bk.v(), func=AF.Sigmoid),
                 reads=[bk.b], writes=[sg.b])
            eng = "dve" if cb % 2 == 0 else "pool"
            P.op(eng, lambda e, tt=tt, cb=cb, sg=sg: e.tensor_tensor(
                out=Utm.v(tt * 2048 + cb * 512, [[1, 512]]), in0=Utm.v(tt * 2048 + cb * 512, [[1, 512]]), in1=sg.v(),
                op=ALU.mult), reads=[Utm.parts[tt], sg.b], writes=[Utm.parts[tt]])
    P.barrier()
    P.flush(C.block)
    es.close()

    es = esL.enter_context(contextlib.ExitStack())
    sb = lambda name, n, dt, nparts=1: C.sb(es, name, n, dt, nparts)
    Wz = sb("Wz5", 8 * 2048, BF16)
    Wo = sb("Wo5", 16 * 1024, BF16)
    xts = [sb("xtC%d" % i, 1024, F32) for i in range(2)]
    xn = sb("xnC", 1024, F32)
    hT = sb("hTC", 1024, BF16, nparts=2)
    szs = [sb("szC%d" % i, 512, F32) for i in range(2)]
    gated = sb("gated", 2048, BF16, nparts=4)
    gT = sb("gT", 2048, BF16, nparts=4)
    xo = sb("xoC", 1024, F32)
    if final is not None:
        fng = sb("fng5", 1024, F32)
        yo = sb("yo5", 1024, F32)
        P.op("sp", lambda e: e.dma_start(out=fng.v(), in_=bass.AP(final, 0, [[1024, 128], [1, 1024]])),
             writes=[fng.b], dma=True)
    wzap = lambda r0, c0, cn: bass.AP(win, r0 * 4096 + c0, [[4096, 128], [1, cn]])
    load_w_cast(C, Wz, wzap, 8, 2048, 2048, None, None)
    wout = W["w_out"]
    woap = lambda r0, c0, cn: bass.AP(wout, r0 * 1024 + c0, [[1024, 128], [1, cn]])
    load_w_cast(C, Wo, woap, 16, 1024, 0, None, None)
    for tt in range(16):
        xt = xts[tt % 2]
        P.op("sp", lambda e, tt=tt, xt=xt: e.dma_start(out=xt.v(), in_=xrow_ap(x_src, tt)), writes=[xt.b], dma=True)
        rms_prep(C, xt, xn, ssq, var, sd, rstd, D)
        make_hT(C, xn, hT, gcol, ident)
        for cb in range(4):
            sz = szs[cb % 2]
            bz = C.bank()
            for kc in range(8):
                P.op("pe", lambda e, kc=kc, cb=cb, bz=bz: e.matmul(
                    bz.v(), lhsT=hT.v(kc * 128, [[1, 128]]), rhs=Wz.v(kc * 2048 + cb * 512, [[1, 512]]),
                    start=(kc == 0), stop=(kc == 7)), reads=hT.parts + [Wz.b], writes=[bz.b])
            P.op("act", lambda e, bz=bz, sz=sz: e.activation(out=sz.v(), in_=bz.v(), func=AF.Silu),
                 reads=[bz.b], writes=[sz.b])
            eng = "dve" if cb % 2 == 0 else "pool"
            P.op(eng, lambda e, tt=tt, cb=cb, sz=sz: e.tensor_tensor(
                out=gated.v(cb * 512, [[1, 512]]), in0=Utm.v(tt * 2048 + cb * 512, [[1, 512]]), in1=sz.v(), op=ALU.mult),
                reads=[Utm.parts[tt], sz.b], writes=[gated.parts[cb]])
        for q4 in range(4):
            bt = C.bank()
            for qq in range(4):
                ec = q4 * 4 + qq
                P.op("pe", lambda e, ec=ec, qq=qq, bt=bt: e.transpose(out=_bf_ap(bt, qq * 128, [[1, 128]], 0, 128),
                                                                     in_=gated.v(ec * 128, [[1, 128]]), identity=identb.v()),
                     reads=[gated.parts[ec // 4], identb.b], writes=[bt.b])
            if q4 % 2 == 0:
                P.op("act", lambda e, q4=q4, bt=bt: e.activation(out=gT.v(q4 * 512, [[1, 512]]),
                                                                in_=_bf_ap(bt, 0, [[1, 512]], 0, 128), func=AF.Copy),
                     reads=[bt.b], writes=[gT.parts[q4]])
            else:
                P.op("dve", lambda e, q4=q4, bt=bt: e.tensor_copy(out=gT.v(q4 * 512, [[1, 512]]),
                                                                 in_=_bf_ap(bt, 0, [[1, 512]], 0, 128)),
                     reads=[bt.b], writes=[gT.parts[q4]])
        for nb in range(2):
            bo2 = C.bank()
            for ec in range(16):
                P.op("pe", lambda e, ec=ec, nb=nb, bo2=bo2: e.matmul(
                    bo2.v(), lhsT=gT.v(ec * 128, [[1, 128]]), rhs=Wo.v(ec * 1024 + nb * 512, [[1, 512]]),
                    start=(ec == 0), stop=(ec == 15)), reads=[gT.parts[ec // 4], Wo.b], writes=[bo2.b])
            P.op("dve", lambda e, nb=nb, bo2=bo2, xt=xt: e.tensor_tensor(
                out=xo.v(nb * 512, [[1, 512]]), in0=bo2.v(), in1=xt.v(nb * 512, [[1, 512]]), op=ALU.add),
                reads=[bo2.b, xt.b], writes=[xo.b])
        emit_out(C, tt, xo, x_dst, final, locals(), row_ap=xrow_ap(x_dst, tt))
    P.barrier()
    P.flush(C.block)
    es.close()
    esL.close()


def gla_host(inputs, j):
    f = lambda a: np.ascontiguousarray(np.asarray(a, dtype=np.float32))
    return {
        "w_in": f(inputs["gla_w_in"][j]),
        "w_out": f(inputs["gla_w_out"][j]),
        "w_gate_up": f(inputs["gla_w_gate_up"][j]),
        "b_gate": f(inputs["gla_b_gate"][j]).reshape(1, 512),
        "norm_col": f(np.asarray(inputs["gla_norm"][j]).reshape(8, 128).T),
        "gain_rep": f(np.broadcast_to(np.asarray(inputs["gla_head_gain"][j]).reshape(1, 512), (128, 512))),
    }


def s5_host(inputs, j):
    f = lambda a: np.ascontiguousarray(np.asarray(a, dtype=np.float32))

    def pair_gp(a):
        a = np.asarray(a)
        sh = a.shape[2:]
        a = a.reshape((64, 2, 64) + sh)
        a = np.moveaxis(a, 0, 2)
        return a.reshape((128, 64) + sh)

    lam_re = pair_gp(inputs["s5_lam_re"][j])
    lam_im = pair_gp(inputs["s5_lam_im"][j])
    logdt = pair_gp(np.broadcast_to(np.asarray(inputs["s5_log_dt"][j]).reshape(128, 1), (128, 64)))
    b_re = pair_gp(inputs["s5_b_re"][j]).reshape(128, 1024)
    b_im = pair_gp(inputs["s5_b_im"][j]).reshape(128, 1024)
    c_re = pair_gp(np.transpose(np.asarray(inputs["s5_c_re"][j]), (0, 2, 1))).reshape(128, 1024)
    c_im = pair_gp(np.transpose(np.asarray(inputs["s5_c_im"][j]), (0, 2, 1))).reshape(128, 1024)
    d = np.asarray(inputs["s5_d"][j])
    dcol = np.broadcast_to(d.T.reshape(1, 16, 128), (8, 16, 128)).reshape(128, 128)
    rep = lambda v: np.broadcast_to(np.asarray(v, dtype=np.float32).reshape(1, -1), (128, len(v)))
    tau = np.arange(8)
    maskM = (tau[:, None] <= tau[None, :]).astype(np.float32)
    maskM = np.kron(maskM, np.ones((16, 16), dtype=np.float32))
    return {
        "w_in": f(inputs["s5_w_in"][j]), "w_glu": f(inputs["s5_w_glu"][j]), "w_out": f(inputs["s5_w_out"][j]),
        "b_glu": f(inputs["s5_b_glu"][j]).reshape(1, 2048),
        "norm_col": f(np.asarray(inputs["s5_norm"][j]).reshape(8, 128).T),
        "lam_re": f(lam_re), "lam_im": f(lam_im), "logdt": f(logdt),
        "b_re": f(b_re), "b_im": f(b_im), "c_re": f(c_re), "c_im": f(c_im), "dcol": f(dcol),
        "kE": f(rep(7.0 - tau)), "kQ": f(rep(tau - 7.0)),
        "kA": f(rep(16.0 * np.arange(16))), "kB": f(rep(1.0 * np.arange(16))),
        "maskM": f(maskM),
    }


_PROG_CACHE = {}


def run_layers(x, inputs, layers, final):
    key = (tuple(layers), final)
    if key not in _PROG_CACHE:
        _PROG_CACHE[key] = build_program(layers, final)
    nc = _PROG_CACHE[key]
    shared = dict(host_consts())
    if final:
        shared["final_rep"] = np.ascontiguousarray(
            np.broadcast_to(np.asarray(inputs["final_norm"], dtype=np.float32).reshape(1, D), (128, D)))
    for li, (kind, j) in enumerate(layers):
        hw = gla_host(inputs, j) if kind == "gla" else s5_host(inputs, j)
        for k, v in hw.items():
            shared["L%d_%s" % (li, k)] = v
    B = x.shape[0]
    in_maps = []
    for b in range(B):
        m = dict(shared)
        m["x"] = np.ascontiguousarray(x[b])
        in_maps.append(m)
    res = run_bass_kernel_spmd(nc, in_maps, core_ids=list(range(B)))
    if DEBUG:
        global DBG_OUT
        DBG_OUT = {k: res.results[0][k] for k in DBG_NAMES}
    return np.stack([r["y"] for r in res.results], axis=0)


ALL_LAYERS = [("gla", 0), ("s5", 0), ("gla", 1), ("s5", 1)]


def kernel(**inputs):
    x = np.asarray(inputs["x"], dtype=np.float32)
    return run_layers(x, inputs, ALL_LAYERS, True).astype(np.float32)
```
